# Optimizing a Trainium2 kernel written in Bass

```python
import jax, jax.numpy as jnp
from jax import lax
import numpy as np

D_MODEL = 1024
BATCH = 2
SEQ = 8192
DEPTH = 2

HEAD_DIM = 64
A_HEADS = D_MODEL // 128
B_HEADS = D_MODEL // 256
C_HEADS = D_MODEL // 256
A_DIM = A_HEADS * HEAD_DIM
B_DIM = B_HEADS * HEAD_DIM
C_DIM = C_HEADS * HEAD_DIM
D_MIX = A_DIM + B_DIM + C_DIM
IN_SPLITS = (A_DIM, A_DIM, A_DIM, A_HEADS, 2 * B_DIM, B_DIM, B_HEADS, B_HEADS, B_DIM, C_DIM, C_DIM, C_DIM, C_DIM)
IN_WIDTH = 3 * A_DIM + A_HEADS + 4 * B_DIM + 2 * B_HEADS + 4 * C_DIM
Q_BLOCK = 128
MLSTM_CHUNK = 64
HGRN_CHUNK = 16
MLSTM_CONV = 4
FFN_CONV = 3
D_FF = ((8 * D_MODEL // 3 + 255) // 256) * 256
EPS = 1e-6

kernel_name = "hymba_fox_mlstm_hgrn2_trunk"


def _rms(x, g):
    xf = x.astype(jnp.float32)
    y = xf * lax.rsqrt(jnp.mean(xf * xf, axis=-1, keepdims=True) + EPS)
    return (y * g.astype(jnp.float32)).astype(x.dtype)


def _heads(z, h):
    b, t, _ = z.shape
    return z.reshape(b, t, h, -1).transpose(0, 2, 1, 3).astype(jnp.float32)


def _merge(o):
    return o.transpose(0, 2, 1, 3)


def _split_cols(z):
    idx = np.cumsum(IN_SPLITS)[:-1].tolist()
    return jnp.split(z, idx, axis=-1)


def _causal_dwconv(x, w):
    k, c = w.shape
    return lax.conv_general_dilated(x, w[:, None, :], window_strides=(1,), padding=[(k - 1, 0)],
                                    dimension_numbers=('NWC', 'WIO', 'NWC'), feature_group_count=c)


def _fox_attention(q, k, v, c):
    bsz, h, t, d = q.shape
    nb = t // Q_BLOCK
    qb = q.reshape(bsz, h, nb, Q_BLOCK, d).transpose(2, 0, 1, 3, 4)
    cb = c.reshape(bsz, h, nb, Q_BLOCK).transpose(2, 0, 1, 3)
    kpos = jnp.arange(t)
    scale = d ** -0.5

    def one_block(args):
        qi, ci, bi = args
        s = jnp.einsum('bhqd,bhkd->bhqk', qi, k) * scale
        s = s + ci[..., None] - c[:, :, None, :]
        qpos = bi * Q_BLOCK + jnp.arange(Q_BLOCK)
        s = jnp.where(kpos[None, :] <= qpos[:, None], s, -jnp.inf)
        p = jax.nn.softmax(s, axis=-1)
        return jnp.einsum('bhqk,bhkd->bhqd', p, v)

    ob = lax.map(one_block, (qb, cb, jnp.arange(nb)))
    return ob.transpose(1, 2, 0, 3, 4).reshape(bsz, h, t, d)


def _mlstm_chunkwise(q, k, v, i_pre, f_pre):
    bsz, h, t, d = q.shape
    L = MLSTM_CHUNK
    nc = t // L
    k = k * d ** -0.5
    qc = q.reshape(bsz, h, nc, L, d)
    kc = k.reshape(bsz, h, nc, L, d)
    vc = v.reshape(bsz, h, nc, L, d)
    ic = i_pre.reshape(bsz, h, nc, L)
    bc = jnp.cumsum(jax.nn.log_sigmoid(f_pre).reshape(bsz, h, nc, L), axis=-1)
    gc = bc[..., -1]
    ac = gc[..., None] - bc + ic

    def step(carry, inp):
        cs, ns, m = carry
        kj, vj, aj, gj = inp
        m_new = jnp.maximum(gj + m, jnp.max(aj, axis=-1))
        decay = jnp.exp(gj + m - m_new)
        w = jnp.exp(aj - m_new[..., None])
        cs_new = decay[..., None, None] * cs + jnp.einsum('bhl,bhld,bhle->bhde', w, kj, vj)
        ns_new = decay[..., None] * ns + jnp.einsum('bhl,bhld->bhd', w, kj)
        return (cs_new, ns_new, m_new), (cs, ns, m)

    mv = lambda a: jnp.moveaxis(a, 2, 0)
    init = (jnp.zeros((bsz, h, d, d), jnp.float32), jnp.zeros((bsz, h, d), jnp.float32),
            jnp.full((bsz, h), -jnp.inf, jnp.float32))
    _, (c_prev, n_prev, m_prev) = lax.scan(step, init, (mv(kc), mv(vc), mv(ac), gc.transpose(2, 0, 1)))
    c_prev = jnp.moveaxis(c_prev, 0, 2)
    n_prev = jnp.moveaxis(n_prev, 0, 2)
    m_prev = jnp.moveaxis(m_prev, 0, 2)

    tri = jnp.tril(jnp.ones((L, L), bool))
    log_d = jnp.where(tri, bc[..., :, None] - bc[..., None, :] + ic[..., None, :], -jnp.inf)
    m_inter = bc + m_prev[..., None]
    m_out = jnp.maximum(m_inter, jnp.max(log_d, axis=-1))
    sqk = jnp.einsum('bhcld,bhcsd->bhcls', qc, kc) * jnp.exp(log_d - m_out[..., None])
    inter_w = jnp.exp(m_inter - m_out)
    num = jnp.einsum('bhcls,bhcse->bhcle', sqk, vc) + inter_w[..., None] * jnp.einsum('bhcld,bhcde->bhcle', qc, c_prev)
    den = jnp.sum(sqk, axis=-1) + inter_w * jnp.einsum('bhcld,bhcd->bhcl', qc, n_prev)
    out = num / jnp.maximum(jnp.abs(den), jnp.exp(-m_out))[..., None]
    return out.reshape(bsz, h, t, d)


def _hgrn2_chunkwise(q, logf, i):
    bsz, h, t, dk = q.shape
    dv = i.shape[-1]
    L = HGRN_CHUNK
    nc = t // L
    k = -jnp.expm1(logf)
    qc = q.reshape(bsz, h, nc, L, dk)
    kc = k.reshape(bsz, h, nc, L, dk)
    vc = i.reshape(bsz, h, nc, L, dv)
    bc = jnp.cumsum(logf.reshape(bsz, h, nc, L, dk), axis=3)
    blast = bc[:, :, :, -1]
    tri = jnp.tril(jnp.ones((L, L), bool))
    rel = jnp.where(tri[:, :, None], bc[:, :, :, :, None, :] - bc[:, :, :, None, :, :], -jnp.inf)
    att = jnp.einsum('bhctd,bhcsd,bhctsd->bhcts', qc, kc, jnp.exp(rel))
    intra = jnp.einsum('bhcts,bhcse->bhcte', att, vc)

    def step(s, inp):
        kj, vj, bj, blj = inp
        kd = kj * jnp.exp(blj[:, :, None, :] - bj)
        return jnp.exp(blj)[..., None] * s + jnp.einsum('bhld,bhle->bhde', kd, vj), s

    mv = lambda a: jnp.moveaxis(a, 2, 0)
    _, s_prev = lax.scan(step, jnp.zeros((bsz, h, dk, dv), jnp.float32), (mv(kc), mv(vc), mv(bc), mv(blast)))
    s_prev = jnp.moveaxis(s_prev, 0, 2)
    inter = jnp.einsum('bhctd,bhcde->bhcte', qc * jnp.exp(bc), s_prev)
    return (intra + inter).reshape(bsz, h, t, dv)


def _mixer(h, lb, w_in, b_in, a_q_g, a_k_g, b_conv_w, out_g, w_out):
    bsz, t, _ = h.shape
    f32 = jnp.float32
    z = h @ w_in + b_in
    (a_q, a_k, a_v, a_f, b_qk, b_v, b_i, b_f, b_o, c_q, c_f, c_i, c_g) = _split_cols(z)
    qa = _rms(_heads(a_q, A_HEADS), a_q_g)
    ka = _rms(_heads(a_k, A_HEADS), a_k_g)
    ca = jnp.cumsum(jax.nn.log_sigmoid(a_f.astype(f32)), axis=1).transpose(0, 2, 1)
    oa = _fox_attention(qa, ka, _heads(a_v, A_HEADS), ca)
    qk = jax.nn.silu(_causal_dwconv(b_qk, b_conv_w))
    qb, kb = jnp.split(qk, 2, axis=-1)
    ob = _mlstm_chunkwise(_heads(qb, B_HEADS), _heads(kb, B_HEADS), _heads(b_v, B_HEADS),
                          b_i.astype(f32).transpose(0, 2, 1), b_f.astype(f32).transpose(0, 2, 1))
    lbh = lb.reshape(C_HEADS, 1, HEAD_DIM)
    logf = jnp.logaddexp(jnp.log(lbh), jnp.log1p(-lbh) + jax.nn.log_sigmoid(_heads(c_f, C_HEADS)))
    oc = _hgrn2_chunkwise(jax.nn.silu(_heads(c_q, C_HEADS)), logf, _heads(c_i, C_HEADS))
    g_a, g_b, g_c = jnp.split(out_g.astype(f32), [A_DIM, A_DIM + B_DIM])
    ya = _rms(_merge(oa), g_a.reshape(A_HEADS, HEAD_DIM)).reshape(bsz, t, A_DIM)
    yb = jax.nn.sigmoid(b_o.astype(f32)) * _rms(_merge(ob), g_b.reshape(B_HEADS, HEAD_DIM)).reshape(bsz, t, B_DIM)
    yc = jax.nn.silu(c_g.astype(f32)) * _rms(_merge(oc), g_c.reshape(C_HEADS, HEAD_DIM)).reshape(bsz, t, C_DIM)
    y = jnp.concatenate([ya, yb, yc], axis=-1).astype(h.dtype)
    return y @ w_out


def _conv_glu(h, w_up, conv_w, conv_b, w_down):
    u = _causal_dwconv(h @ w_up, conv_w) + conv_b
    gate, up = jnp.split(u, 2, axis=-1)
    return (jax.nn.silu(gate) * up) @ w_down


def setup_inputs(seed: int = 0) -> dict:
    key = jax.random.key(seed)
    ks = jax.random.split(key, 16)
    f32 = jnp.float32
    nrm = lambda k, shape, s: jax.random.normal(k, shape, f32) * s
    gate_offsets = jnp.concatenate([
        jnp.zeros((3 * A_DIM,), f32), jnp.linspace(1.0, 4.0, A_HEADS, dtype=f32),
        jnp.zeros((3 * B_DIM + B_HEADS,), f32), jnp.linspace(3.0, 6.0, B_HEADS, dtype=f32),
        jnp.zeros((B_DIM + 4 * C_DIM,), f32)])
    return {
        "x": nrm(ks[0], (BATCH, SEQ, D_MODEL), 1.0),
        "lb_logits": nrm(ks[1], (DEPTH, C_DIM), 0.5),
        "norm_mix_g": 1.0 + nrm(ks[2], (DEPTH, D_MODEL), 0.02),
        "w_in": nrm(ks[3], (DEPTH, D_MODEL, IN_WIDTH), D_MODEL ** -0.5),
        "b_in": nrm(ks[4], (DEPTH, IN_WIDTH), 0.02) + gate_offsets,
        "a_q_g": 1.0 + nrm(ks[5], (DEPTH, HEAD_DIM), 0.02),
        "a_k_g": 1.0 + nrm(ks[6], (DEPTH, HEAD_DIM), 0.02),
        "b_conv_w": nrm(ks[7], (DEPTH, MLSTM_CONV, 2 * B_DIM), MLSTM_CONV ** -0.5),
        "out_g": 1.0 + nrm(ks[8], (DEPTH, D_MIX), 0.02),
        "w_out": nrm(ks[9], (DEPTH, D_MIX, D_MODEL), D_MIX ** -0.5),
        "norm_ffn_g": 1.0 + nrm(ks[10], (DEPTH, D_MODEL), 0.02),
        "w_up": nrm(ks[11], (DEPTH, D_MODEL, 2 * D_FF), D_MODEL ** -0.5),
        "ffn_conv_w": nrm(ks[12], (DEPTH, FFN_CONV, 2 * D_FF), FFN_CONV ** -0.5),
        "ffn_conv_b": nrm(ks[13], (DEPTH, 2 * D_FF), 0.02),
        "w_down": nrm(ks[14], (DEPTH, D_FF, D_MODEL), D_FF ** -0.5),
    }


def reference(x, lb_logits, norm_mix_g, w_in, b_in, a_q_g, a_k_g, b_conv_w, out_g, w_out,
              norm_ffn_g, w_up, ffn_conv_w, ffn_conv_b, w_down):
    p = jax.nn.softmax(lb_logits.astype(jnp.float32), axis=0)
    lb_all = jnp.maximum(jnp.cumsum(p, axis=0) - p[0], 0.0)
    for l in range(DEPTH):
        h = _rms(x, norm_mix_g[l])
        x = x + _mixer(h, lb_all[l], w_in[l], b_in[l], a_q_g[l], a_k_g[l], b_conv_w[l], out_g[l], w_out[l])
        h = _rms(x, norm_ffn_g[l])
        x = x + _conv_glu(h, w_up[l], ffn_conv_w[l], ffn_conv_b[l], w_down[l])
    return x
```

```python
import contextlib
import numpy as np
import concourse.bass as bass
import concourse.mybir as mybir
from concourse.bass_utils import run_bass_kernel_spmd

F32 = mybir.dt.float32
BF16 = mybir.dt.bfloat16
ALU = mybir.AluOpType
AF = mybir.ActivationFunctionType
AX = mybir.AxisListType

SEQ = 8192
DM = 1024
DFF = 2816
EPS = 1e-6


class Res:
    __slots__ = ("name", "writer", "readers")

    def __init__(self, name):
        self.name = name
        self.writer = None
        self.readers = []


class Tl:
    def __init__(self, t, name):
        self.t = t
        self.r = Res(name)

    def __getitem__(self, k):
        return self.t[k]


class _View:
    def __init__(self, ap):
        self.ap = ap

    def __getitem__(self, k):
        return self.ap[k]


class Op:
    __slots__ = ("eng", "fn", "deps", "needed", "token", "is_dma", "sem", "prev_token")


def _res(x):
    return x.r if isinstance(x, Tl) else x


class Prog:
    ENGS = ["sync", "scalar", "vector", "gpsimd", "tensor"]
    NDMA = 8

    def __init__(self, nc):
        self.nc = nc
        self.ops = {e: [] for e in self.ENGS}

    def add(self, eng, fn, reads=(), writes=(), dma=False):
        op = Op()
        op.eng = eng
        op.fn = fn
        op.is_dma = dma
        op.needed = dma
        op.token = 0
        op.sem = None
        op.prev_token = 0
        reads = [_res(r) for r in reads]
        writes = [_res(w) for w in writes]
        deps = []
        for r in reads:
            if r.writer is not None:
                deps.append(r.writer)
        for w in writes:
            if w.writer is not None:
                deps.append(w.writer)
            deps.extend(w.readers)
        seen = set()
        op.deps = []
        for d in deps:
            if id(d) in seen:
                continue
            seen.add(id(d))
            if eng == "tensor" and d.eng == "tensor" and not d.is_dma and not dma:
                continue
            op.deps.append(d)
            d.needed = True
        for r in reads:
            r.readers.append(op)
        for w in writes:
            w.writer = op
            w.readers = []
        self.ops[eng].append(op)
        return op

    def dma(self, out, in_, reads=(), writes=(), eng="sync", **kw):
        return self.add(eng, lambda e: e.dma_start(out=out, in_=in_, **kw), reads, writes, dma=True)

    def barrier(self):
        lasts = []
        for e in self.ENGS:
            ops = self.ops[e]
            nd = 0
            got_c = False
            for op in reversed(ops):
                if op.fn is None:
                    break
                if op.is_dma:
                    if nd < self.NDMA:
                        lasts.append(op)
                        nd += 1
                elif not got_c:
                    lasts.append(op)
                    got_c = True
                if got_c and nd >= self.NDMA:
                    break
        for e in self.ENGS:
            op = Op()
            op.eng = e
            op.fn = None
            op.is_dma = False
            op.needed = False
            op.token = 0
            op.sem = None
            op.prev_token = 0
            op.deps = [d for d in lasts]
            for d in op.deps:
                d.needed = True
            self.ops[e].append(op)

    def emit(self, final_wait_eng="sync"):
        nc = self.nc
        with contextlib.ExitStack() as st:
            esem = {e: st.enter_context(nc.semaphore("s_" + e)) for e in self.ENGS}
            dsem = {e: [st.enter_context(nc.semaphore("d_%s_%d" % (e, i))) for i in range(self.NDMA)]
                    for e in self.ENGS if any(o.is_dma for o in self.ops[e])}
            semobj = {}
            last_dma = {}
            for e in self.ENGS:
                cnt = 0
                k = 0
                for op in self.ops[e]:
                    if op.is_dma:
                        si = k % self.NDMA
                        op.sem = ("d", e, si)
                        semobj[op.sem] = dsem[e][si]
                        op.token = 16 * (k // self.NDMA + 1)
                        op.prev_token = 16 * (k // self.NDMA)
                        last_dma[op.sem] = op.token
                        k += 1
                    elif op.needed:
                        cnt += 1
                        op.sem = ("e", e)
                        semobj[op.sem] = esem[e]
                        op.token = cnt
            block = st.enter_context(nc.Block())
            prog = self

            def run(e, eng):
                waited = {}
                for op in prog.ops[e]:
                    waits = {}
                    for d in op.deps:
                        if waits.get(d.sem, 0) < d.token:
                            waits[d.sem] = d.token
                    if op.is_dma and op.prev_token > 0:
                        if waits.get(op.sem, 0) < op.prev_token:
                            waits[op.sem] = op.prev_token
                    for s, v in waits.items():
                        if waited.get(s, 0) < v:
                            eng.wait_ge(semobj[s], v)
                            waited[s] = v
                    if op.fn is None:
                        continue
                    ins = op.fn(eng)
                    if op.is_dma:
                        ins.then_inc(semobj[op.sem], 16)
                    elif op.needed:
                        ins.then_inc(semobj[op.sem], 1)
                if e == final_wait_eng:
                    for s, v in last_dma.items():
                        if waited.get(s, 0) < v:
                            eng.wait_ge(semobj[s], v)
                            waited[s] = v

            @block.sync
            def _(eng):
                run("sync", eng)

            @block.scalar
            def _(eng):
                run("scalar", eng)

            @block.vector
            def _(eng):
                run("vector", eng)

            @block.gpsimd
            def _(eng):
                run("gpsimd", eng)

            @block.tensor
            def _(eng):
                run("tensor", eng)


class Ctx:
    N = 0

    def __init__(self, nc, st):
        self.nc = nc
        self.st = st
        self.n = 0

    def sb(self, name, shape, dt=F32, side=None):
        Ctx.N += 1
        return Tl(self.st.enter_context(self.nc.sbuf_tensor("%s_%d" % (name, Ctx.N), list(shape), dt, side=side)), name)

    def ps(self, name, shape, dt=F32):
        Ctx.N += 1
        full = 512 if dt == F32 else 1024
        t = self.st.enter_context(self.nc.psum_tensor("%s_%d" % (name, Ctx.N), [128, full], dt))
        p = shape[0]
        if len(shape) == 2:
            v = t[0:p, 0:shape[1]]
        else:
            assert len(shape) == 3
            v = t[0:p, 0:shape[1] * shape[2]].rearrange("p (a b) -> p a b", b=shape[2])
        return Tl(_View(v), name)


NTM = 644
C_AQ, C_AK, C_AV, C_BV, C_BO, C_CI, C_CG, C_G = 0, 128, 256, 384, 448, 512, 576, 640


def build_mixer(T=SEQ, stop=99):
    NT = T // 128
    nc = bass.Bass("TRN2", target_bir_lowering=False)

    def din(name, shape):
        return nc.dram_tensor(name, list(shape), F32, kind="ExternalInput").ap()

    xb = din("xb", [T, DM])
    wtm_d = din("wtm", [DM, NTM])
    wfm_d = din("wfm", [DM, 256])
    gmix_d = din("gmix", [128, 8])
    btm_d = din("btm", [128, NTM])
    bfm_d = din("bfm", [64, 512])
    gqk_d = din("gqk", [128, 256])
    convw_d = din("convw", [64, 8])
    gout_d = din("gout", [128, 256])
    lbl_d = din("lbl", [64, 2])
    lsel_d = din("lsel", [64, 1])
    ident_d = din("ident", [128, 128])
    cmask_d = din("cmask", [128, 128])
    bdmask_d = din("bdmask", [128, 128])
    ltri_d = din("ltri", [64, 64])
    rmask_d = din("rmask", [64, 2048])
    hmask_d = din("hmask", [64, 2048])
    yT = nc.dram_tensor("yT", [256, T], F32, kind="ExternalOutput").ap()
    zfm_d = nc.dram_tensor("zfm_s", [4, 64, T + 4], F32).ap()
    rows_d = nc.dram_tensor("rows_s", [16, T], F32).ap()
    rowsb_d = nc.dram_tensor("rowsb_s", [16, T], BF16).ap()
    r_zfm = Res("zfm_d")
    r_rows = Res("rows_d")
    r_rowsb = Res("rowsb_d")
    r_yT = Res("yT")

    P = Prog(nc)
    with contextlib.ExitStack() as st_all:
        G = Ctx(nc, st_all)
        identf = G.sb("identf", [128, 128])
        identb = G.sb("identb", [128, 128], BF16)
        cmask = G.sb("cmask", [128, 128])
        cmaskb = G.sb("cmaskb", [128, 128], BF16)
        bdmask = G.sb("bdmask", [128, 128])
        ltri = G.sb("ltri", [64, 64])
        gout = G.sb("gout", [128, 256])
        ones_f = G.sb("ones_f", [128, 128])
        P.dma(identf[:], ident_d[:, :], writes=[identf])
        P.dma(cmask[:], cmask_d[:, :], writes=[cmask])
        P.dma(bdmask[:], bdmask_d[:, :], writes=[bdmask])
        P.dma(ltri[:], ltri_d[:, :], writes=[ltri])
        P.dma(gout[:], gout_d[:, :], writes=[gout])
        P.add("vector", lambda e: e.tensor_copy(out=identb[:], in_=identf[:]), [identf], [identb])
        P.add("vector", lambda e: e.tensor_copy(out=cmaskb[:], in_=cmask[:]), [cmask], [cmaskb])
        P.add("vector", lambda e: e.memset(ones_f[:], 1.0), [], [ones_f])

        st_fox = contextlib.ExitStack()
        Cf = Ctx(nc, st_fox)
        QK = Cf.sb("QK", [70, 4, T], BF16, side="right")
        VA = Cf.sb("VA", [128, NT, 2, 65], BF16, side="right")
        VB = G.sb("VB", [128, NT, 65], BF16)
        VC = G.sb("VC", [128, NT, 64], BF16)
        GB = G.sb("GB", [128, NT, 64], BF16)
        GC = G.sb("GC", [128, NT, 64], BF16)
        GT = G.sb("GT", [128, NT, 4])
        P.add("gpsimd", lambda e: e.memset(VA[:], 1.0), [], [VA])
        P.add("gpsimd", lambda e: e.memset(VB[:], 1.0), [], [VB])

        with contextlib.ExitStack() as st1:
            C1 = Ctx(nc, st1)
            wtm = C1.sb("wtm", [128, 8, NTM], BF16)
            wfm = C1.sb("wfm", [128, 8, 256], BF16)
            gmix = C1.sb("gmix", [128, 8])
            btm = C1.sb("btm", [128, NTM])
            bfm = C1.sb("bfm", [64, 512])
            gqk = C1.sb("gqk", [128, 256])
            wst = [C1.sb("wst%d" % i, [128, NTM]) for i in range(2)]
            P.dma(gmix[:], gmix_d[:, :], writes=[gmix])
            P.dma(btm[:], btm_d[:, :], writes=[btm])
            P.dma(bfm[:], bfm_d[:, :], writes=[bfm])
            P.dma(gqk[:], gqk_d[:, :], writes=[gqk])
            for kc in range(8):
                s = wst[kc % 2]
                P.dma(s[:], wtm_d[kc * 128:(kc + 1) * 128, :], writes=[s])
                P.add("vector", lambda e, s=s, kc=kc: e.tensor_scalar(
                    out=wtm[:, kc, :], in0=s[:], scalar1=gmix[:, kc:kc + 1], scalar2=None, op0=ALU.mult),
                    [s, gmix], [wtm])
            for kc in range(8):
                s = wst[kc % 2]
                P.dma(s[:, 0:256], wfm_d[kc * 128:(kc + 1) * 128, :], writes=[s])
                P.add("vector", lambda e, s=s, kc=kc: e.tensor_scalar(
                    out=wfm[:, kc, :], in0=s[:, 0:256], scalar1=gmix[:, kc:kc + 1], scalar2=None, op0=ALU.mult),
                    [s, gmix], [wfm])
            zpad = C1.sb("zpad", [64, 4, 4])
            P.add("vector", lambda e: e.memset(zpad[:], 0.0), [], [zpad])
            P.dma(zfm_d[:, :, 0:4].rearrange("j d t -> d j t"), zpad[:], reads=[zpad], writes=[r_zfm])

            xt = [C1.sb("xt%d" % i, [128, DM]) for i in range(2)]
            sqj = C1.sb("sqj", [128, DM])
            ss = [C1.sb("ss%d" % i, [128, 1]) for i in range(2)]
            xn = [C1.sb("xn%d" % i, [128, DM], BF16) for i in range(2)]
            xnT = [C1.sb("xnT%d" % i, [128, DM], BF16) for i in range(2)]
            zsb = [C1.sb("zsb%d" % i, [128, NTM]) for i in range(2)]
            zfs = [C1.sb("zfs%d" % i, [64, 4, 128]) for i in range(2)]
            sq2 = C1.sb("sq2", [128, 256])
            ss4 = [C1.sb("ss4%d" % i, [128, 4]) for i in range(2)]
            qkg = [C1.sb("qkg%d" % i, [128, 256]) for i in range(2)]
            qkn = [C1.sb("qkn%d" % i, [128, 256], BF16) for i in range(2)]
            pT = [C1.ps("pT%d" % i, [128, DM], BF16) for i in range(2)]
            pzA = [C1.ps("pzA%d" % i, [128, 512]) for i in range(2)]
            pzB = C1.ps("pzB", [128, NTM - 512])
            pzF = C1.ps("pzF", [64, 4, 128])
            pqT = C1.ps("pqT", [64, 4, 128], BF16)

            def load_x(t):
                P.dma(xt[t % 2][:], xb[t * 128:(t + 1) * 128, :], writes=[xt[t % 2]])

            load_x(0)
            for t in range(NT):
                b = t % 2
                if t + 1 < NT:
                    load_x(t + 1)
                X, SS, XN, XT, Z, ZF, S4, QG, QN = xt[b], ss[b], xn[b], xnT[b], zsb[b], zfs[b], ss4[b], qkg[b], qkn[b]
                PT, PA = pT[b], pzA[b]
                P.add("scalar", lambda e, X=X, SS=SS: e.activation(out=sqj[:], in_=X[:], func=AF.Square, accum_out=SS[:]),
                      [X], [sqj, SS])
                P.add("vector", lambda e, SS=SS: e.tensor_scalar(out=SS[:], in0=SS[:], scalar1=1.0 / DM, scalar2=EPS,
                                                                 op0=ALU.mult, op1=ALU.add), [SS], [SS])
                P.add("scalar", lambda e, SS=SS: e.activation(out=SS[:], in_=SS[:], func=AF.Sqrt), [SS], [SS])
                P.add("vector", lambda e, SS=SS: e.reciprocal(out=SS[:], in_=SS[:]), [SS], [SS])
                P.add("vector", lambda e, X=X, SS=SS, XN=XN: e.tensor_scalar(
                    out=XN[:], in0=X[:], scalar1=SS[:, 0:1], scalar2=None, op0=ALU.mult), [X, SS], [XN])
                for kc in range(8):
                    P.add("tensor", lambda e, kc=kc, XN=XN, PT=PT: e.transpose(
                        out=PT[:, kc * 128:(kc + 1) * 128], in_=XN[:, kc * 128:(kc + 1) * 128], identity=identb[:]),
                        [XN, identb], [PT])
                P.add("scalar", lambda e, XT=XT, PT=PT: e.copy(out=XT[:], in_=PT[:]), [PT], [XT])
                for kc in range(8):
                    P.add("tensor", lambda e, kc=kc, XT=XT, PA=PA: e.matmul(
                        out=PA[:], lhsT=XT[:, kc * 128:(kc + 1) * 128], rhs=wtm[:, kc, 0:512],
                        start=(kc == 0), stop=(kc == 7)), [XT, wtm], [PA])
                for kc in range(8):
                    P.add("tensor", lambda e, kc=kc, XT=XT: e.matmul(
                        out=pzB[:], lhsT=XT[:, kc * 128:(kc + 1) * 128], rhs=wtm[:, kc, 512:NTM],
                        start=(kc == 0), stop=(kc == 7)), [XT, wtm], [pzB])
                for j in range(4):
                    for kc in range(8):
                        P.add("tensor", lambda e, kc=kc, j=j, XT=XT: e.matmul(
                            out=pzF[:, j, :], lhsT=wfm[:, kc, j * 64:(j + 1) * 64], rhs=XT[:, kc * 128:(kc + 1) * 128],
                            start=(kc == 0), stop=(kc == 7)), [XT, wfm], [pzF])
                P.add("vector", lambda e, Z=Z, PA=PA: e.tensor_tensor(out=Z[:, 0:512], in0=PA[:], in1=btm[:, 0:512], op=ALU.add),
                      [PA, btm], [Z])
                P.add("vector", lambda e, Z=Z: e.tensor_tensor(out=Z[:, 512:NTM], in0=pzB[:], in1=btm[:, 512:NTM], op=ALU.add),
                      [pzB, btm, Z], [Z])
                P.add("vector", lambda e, ZF=ZF: e.tensor_tensor(out=ZF[:].rearrange("d j t -> d (j t)"),
                                                                 in0=pzF[:].rearrange("d j t -> d (j t)"),
                                                                 in1=bfm[:], op=ALU.add), [pzF, bfm], [ZF])
                P.dma(zfm_d[:, :, 4 + t * 128:4 + (t + 1) * 128].rearrange("j d t -> d j t"), ZF[:], reads=[ZF], writes=[r_zfm])
                P.add("gpsimd", lambda e, Z=Z: e.tensor_tensor(out=sq2[:], in0=Z[:, 0:256], in1=Z[:, 0:256], op=ALU.mult),
                      [Z], [sq2])
                P.add("vector", lambda e, S4=S4: e.tensor_reduce(out=S4[:], in_=sq2[:].rearrange("p (a d) -> p a d", d=64),
                                                                 axis=AX.X, op=ALU.add), [sq2], [S4])
                P.add("vector", lambda e, S4=S4: e.tensor_scalar(out=S4[:], in0=S4[:], scalar1=1.0 / 64, scalar2=EPS,
                                                                 op0=ALU.mult, op1=ALU.add), [S4], [S4])
                P.add("scalar", lambda e, S4=S4: e.activation(out=S4[:], in_=S4[:], func=AF.Sqrt), [S4], [S4])
                P.add("vector", lambda e, S4=S4: e.reciprocal(out=S4[:], in_=S4[:]), [S4], [S4])
                P.add("vector", lambda e, S4=S4: e.tensor_scalar(out=S4[:, 2:4], in0=S4[:, 2:4], scalar1=0.125, scalar2=None,
                                                                 op0=ALU.mult), [S4], [S4])
                P.add("gpsimd", lambda e, Z=Z, QG=QG: e.tensor_tensor(out=QG[:], in0=Z[:, 0:256], in1=gqk[:], op=ALU.mult),
                      [Z, gqk], [QG])
                for a in range(4):
                    P.add("scalar", lambda e, a=a, QG=QG, QN=QN, S4=S4: e.activation(
                        out=QN[:, a * 64:(a + 1) * 64], in_=QG[:, a * 64:(a + 1) * 64], func=AF.Copy, scale=S4[:, a:a + 1]),
                        [QG, S4], [QN])
                for a in range(4):
                    P.add("tensor", lambda e, a=a, QN=QN: e.transpose(out=pqT[:, a, :], in_=QN[:, a * 64:(a + 1) * 64],
                                                                     identity=identb[:]), [QN, identb], [pqT])
                P.add("vector", lambda e, t=t: e.tensor_copy(out=QK[0:64, :, t * 128:(t + 1) * 128], in_=pqT[:]), [pqT], [QK])
                P.add("gpsimd", lambda e, t=t, Z=Z: e.tensor_copy(
                    out=VA[:, t, :, 0:64], in_=Z[:, C_AV:C_AV + 128].rearrange("p (h d) -> p h d", d=64)), [Z], [VA])
                P.add("gpsimd", lambda e, t=t, Z=Z: e.tensor_copy(out=VB[:, t, 0:64], in_=Z[:, C_BV:C_BV + 64]), [Z], [VB])
                P.add("gpsimd", lambda e, t=t, Z=Z: e.tensor_copy(out=VC[:, t, :], in_=Z[:, C_CI:C_CI + 64]), [Z], [VC])
                P.add("gpsimd", lambda e, t=t, Z=Z: e.tensor_copy(out=GT[:, t, :], in_=Z[:, C_G:C_G + 4]), [Z], [GT])
                P.add("scalar", lambda e, t=t, Z=Z: e.activation(out=GB[:, t, :], in_=Z[:, C_BO:C_BO + 64], func=AF.Sigmoid),
                      [Z], [GB])
                P.add("scalar", lambda e, t=t, Z=Z: e.activation(out=GC[:, t, :], in_=Z[:, C_CG:C_CG + 64], func=AF.Silu),
                      [Z], [GC])
        P.barrier()
        if stop == 1:
            P.emit()
            return nc

        IW = G.sb("IW", [128, NT])
        SW = G.sb("SW", [128, NT])
        DEC = G.sb("DEC", [128, NT])
        EM = G.sb("EM", [128, NT])
        with contextlib.ExitStack() as st2:
            C2 = Ctx(nc, st2)
            LS = C2.sb("LS", [128, NT, 4])
            zer = C2.sb("zer", [64, 128])
            P.add("vector", lambda e: e.memset(zer[:], 0.0), [], [zer])
            P.add("scalar", lambda e: e.activation(out=LS[:], in_=GT[:], func=AF.Sigmoid), [GT], [LS])
            P.add("scalar", lambda e: e.activation(out=LS[:], in_=LS[:], func=AF.Ln), [LS], [LS])
            pcm = C2.ps("pcm", [64, 128])
            pcol = C2.ps("pcol", [64, 1])

            def to_cm(src_tl, col, dst):
                P.add("tensor", lambda e: e.transpose(out=pcm[0:NT, :], in_=src_tl[:, :, col], identity=identf[:]),
                      [src_tl, identf], [pcm])
                P.add("vector", lambda e: e.tensor_copy(out=dst[0:NT, :], in_=pcm[0:NT, :]), [pcm], [dst])

            def cumsum_cm(lf, out):
                P.add("vector", lambda e: e.tensor_tensor_scan(out=out[0:NT, :], data0=lf[0:NT, :], data1=zer[0:NT, :], initial=0.0,
                                                               op0=ALU.add, op1=ALU.add), [lf, zer], [out])
                P.add("tensor", lambda e: e.matmul(out=pcol[0:NT, :], lhsT=ltri[0:NT, 0:NT], rhs=out[0:NT, 127:128], start=True, stop=True),
                      [ltri, out], [pcol])
                offs = C2.sb("offs", [64, 1])
                P.add("vector", lambda e: e.tensor_copy(out=offs[0:NT, :], in_=pcol[0:NT, :]), [pcol], [offs])
                P.add("vector", lambda e: e.tensor_scalar(out=out[0:NT, :], in0=out[0:NT, :], scalar1=offs[0:NT, 0:1], scalar2=None,
                                                          op0=ALU.add), [out, offs], [out])

            for h in range(2):
                lf = C2.sb("lf", [64, 128])
                c = C2.sb("c", [64, 128])
                to_cm(LS, h, lf)
                cumsum_cm(lf, c)
                parts = []
                rem = c
                for i in range(3):
                    pb = C2.sb("pb", [64, 128], BF16)
                    P.add("vector", lambda e, pb=pb, rem=rem: e.tensor_copy(out=pb[0:NT, :], in_=rem[0:NT, :]), [rem], [pb])
                    parts.append(pb)
                    if i < 2:
                        nr = C2.sb("nr", [64, 128])
                        P.add("vector", lambda e, nr=nr, rem=rem, pb=pb: e.tensor_tensor(
                            out=nr[0:NT, :], in0=rem[0:NT, :], in1=pb[0:NT, :], op=ALU.subtract), [rem, pb], [nr])
                        rem = nr
                for i in range(3):
                    pb = parts[i]
                    nb = C2.sb("nb", [64, 128], BF16)
                    P.add("vector", lambda e, nb=nb, pb=pb: e.tensor_scalar(out=nb[0:NT, :], in0=pb[0:NT, :], scalar1=-1.0,
                                                                            scalar2=None, op0=ALU.mult), [pb], [nb])
                    rq = h * 8 + i
                    rk = h * 8 + 3 + i
                    P.dma(rowsb_d[rq, :].rearrange("(c t) -> c t", t=128), pb[0:NT, :], reads=[pb], writes=[r_rowsb])
                    P.dma(rowsb_d[rk, :].rearrange("(c t) -> c t", t=128), nb[0:NT, :], reads=[nb], writes=[r_rowsb])
            oneb = C2.sb("oneb", [1, T], BF16)
            P.add("gpsimd", lambda e: e.memset(oneb[:], 1.0), [], [oneb])
            for h in range(2):
                for i in range(3):
                    P.dma(QK[64 + i:65 + i, h, :], rowsb_d[h * 8 + i:h * 8 + i + 1, :], reads=[r_rowsb], writes=[QK])
                    P.dma(QK[67 + i:68 + i, 2 + h, :], rowsb_d[h * 8 + 3 + i:h * 8 + 4 + i, :], reads=[r_rowsb], writes=[QK])
                    P.dma(QK[67 + i:68 + i, h, :], oneb[:], reads=[oneb], writes=[QK])
                    P.dma(QK[64 + i:65 + i, 2 + h, :], oneb[:], reads=[oneb], writes=[QK])

            lfb = C2.sb("lfb", [64, 128])
            Bc = C2.sb("Bc", [64, 128])
            ipre = C2.sb("ipre", [64, 128])
            u = C2.sb("u", [64, 128])
            Ul = C2.sb("Ul", [64, 128])
            to_cm(LS, 3, lfb)
            cumsum_cm(lfb, Bc)
            to_cm(GT, 2, ipre)
            P.add("vector", lambda e: e.tensor_tensor(out=u[0:NT, :], in0=ipre[0:NT, :], in1=Bc[0:NT, :], op=ALU.subtract),
                  [ipre, Bc], [u])
            P.add("vector", lambda e: e.tensor_tensor_scan(out=Ul[0:NT, :], data0=u[0:NT, :], data1=u[0:NT, :], initial=-1e30,
                                                           op0=ALU.max, op1=ALU.max), [u], [Ul])
            prow = C2.ps("prow", [1, 64])
            mrow = C2.sb("mrow", [1, 64])
            irow = C2.sb("irow", [1, 64])
            Rrow = C2.sb("Rrow", [1, 66])
            P.add("tensor", lambda e: e.transpose(out=prow[0:1, 0:NT], in_=Ul[0:NT, 127:128], identity=identf[0:NT, 0:NT]),
                  [Ul, identf], [prow])
            P.add("vector", lambda e: e.tensor_copy(out=mrow[0:1, 0:NT], in_=prow[0:1, 0:NT]), [prow], [mrow])
            P.add("vector", lambda e: e.tensor_tensor_scan(out=irow[0:1, 0:NT], data0=mrow[0:1, 0:NT], data1=mrow[0:1, 0:NT],
                                                           initial=-1e30, op0=ALU.max, op1=ALU.max), [mrow], [irow])
            P.add("vector", lambda e: e.memset(Rrow[:], -1e4), [], [Rrow])
            P.add("vector", lambda e: e.tensor_copy(out=Rrow[0:1, 1:NT + 1], in_=irow[0:1, 0:NT]), [irow, Rrow], [Rrow])
            pRb = C2.ps("pRb", [128, 66])
            Rb = C2.sb("Rb", [128, 66])
            P.add("tensor", lambda e: e.matmul(out=pRb[:, 0:NT + 1], lhsT=ones_f[0:1, :], rhs=Rrow[0:1, 0:NT + 1], start=True, stop=True),
                  [ones_f, Rrow], [pRb])
            P.add("vector", lambda e: e.tensor_copy(out=Rb[:, 0:NT + 1], in_=pRb[:, 0:NT + 1]), [pRb], [Rb])
            Ucm = C2.sb("Ucm", [64, 128])
            pRc = C2.ps("pRc", [64, 1])
            Rcol = C2.sb("Rcol", [64, 1])
            P.add("tensor", lambda e: e.transpose(out=pRc[0:NT, :], in_=Rrow[0:1, 0:NT], identity=identf[0:1, 0:1]),
                  [Rrow, identf], [pRc])
            P.add("vector", lambda e: e.tensor_copy(out=Rcol[0:NT, :], in_=pRc[0:NT, :]), [pRc], [Rcol])
            P.add("vector", lambda e: e.tensor_scalar(out=Ucm[0:NT, :], in0=Ul[0:NT, :], scalar1=Rcol[0:NT, 0:1], scalar2=None, op0=ALU.max),
                  [Ul, Rcol], [Ucm])
            nU = C2.sb("nU", [64, 128])
            P.add("vector", lambda e: e.tensor_scalar(out=nU[0:NT, :], in0=Ucm[0:NT, :], scalar1=-1.0, scalar2=None, op0=ALU.mult),
                  [Ucm], [nU])
            P.dma(rows_d[0, :].rearrange("(c t) -> c t", t=128), u[0:NT, :], reads=[u], writes=[r_rows])
            P.dma(rows_d[1, :].rearrange("(c t) -> c t", t=128), nU[0:NT, :], reads=[nU], writes=[r_rows])
            ptm = C2.ps("ptm", [128, 64])

            def to_tm(src, dst):
                P.add("tensor", lambda e: e.transpose(out=ptm[:, 0:NT], in_=src[0:NT, :], identity=identf[0:NT, 0:NT]),
                      [src, identf], [ptm])
                P.add("vector", lambda e: e.tensor_copy(out=dst[:, 0:NT], in_=ptm[:, 0:NT]), [ptm], [dst])

            u_tm = C2.sb("u_tm", [128, 64])
            U_tm = C2.sb("U_tm", [128, 64])
            B_tm = C2.sb("B_tm", [128, 64])
            to_tm(u, u_tm)
            to_tm(Ucm, U_tm)
            to_tm(Bc, B_tm)
            P.add("vector", lambda e: e.tensor_tensor(out=IW[:], in0=Rb[:, 0:NT], in1=U_tm[:, 0:NT], op=ALU.subtract), [Rb, U_tm], [IW])
            P.add("scalar", lambda e: e.activation(out=IW[:], in_=IW[:], func=AF.Exp), [IW], [IW])
            P.add("vector", lambda e: e.tensor_tensor(out=SW[:], in0=u_tm[:, 0:NT], in1=Rb[:, 1:NT + 1], op=ALU.subtract), [Rb, u_tm], [SW])
            P.add("scalar", lambda e: e.activation(out=SW[:], in_=SW[:], func=AF.Exp), [SW], [SW])
            P.add("vector", lambda e: e.tensor_tensor(out=DEC[:], in0=Rb[:, 0:NT], in1=Rb[:, 1:NT + 1], op=ALU.subtract), [Rb], [DEC])
            P.add("scalar", lambda e: e.activation(out=DEC[:], in_=DEC[:], func=AF.Exp), [DEC], [DEC])
            P.add("vector", lambda e: e.tensor_tensor(out=EM[:], in0=B_tm[:, 0:NT], in1=U_tm[:, 0:NT], op=ALU.add), [B_tm, U_tm], [EM])
            P.add("scalar", lambda e: e.activation(out=EM[:], in_=EM[:], func=AF.Exp, scale=-1.0), [EM], [EM])
        P.barrier()
        if stop == 2:
            P.emit()
            return nc

        with contextlib.ExitStack() as st3:
            C3 = Ctx(nc, st3)
            pS = [C3.ps("pS%d" % i, [128, 512]) for i in range(2)]
            pO = [C3.ps("pO%d" % i, [65, 512]) for i in range(2)]
            pOT = C3.ps("pOT", [128, 4, 65])
            pYT = C3.ps("pYT", [128, 512], BF16)
            PT_ = [C3.sb("Pt%d" % i, [128, 512], BF16) for i in range(3)]
            Osb = [C3.sb("Osb%d" % i, [65, 512]) for i in range(2)]
            Ot = C3.sb("Ot", [128, 4, 64])
            rd = C3.sb("rd", [128, 4])
            s4 = C3.sb("s4", [128, 4])
            junk = C3.sb("junk", [128, 64])
            Ytm = [C3.sb("Ytm%d" % i, [128, 4, 128], BF16) for i in range(2)]
            Yst = [C3.sb("Yst%d" % i, [128, 512]) for i in range(2)]
            nqt = T // 512
            pi = 0
            for qt in range(nqt):
                Y = Ytm[qt % 2]
                for h in range(2):
                    PO = pO[h]
                    OS = Osb[h]
                    nj = 4 * qt + 4
                    for J in range(nj):
                        sub = max(0, J - 4 * qt)
                        c0 = sub * 128
                        PS = pS[pi % 2]
                        PB = PT_[pi % 3]
                        pi += 1
                        q0 = qt * 512 + c0
                        P.add("tensor", lambda e, PS=PS, h=h, J=J, q0=q0, c0=c0, qt=qt: e.matmul(
                            out=PS[:, c0:512], lhsT=QK[0:70, 2 + h, J * 128:(J + 1) * 128],
                            rhs=QK[0:70, h, q0:(qt + 1) * 512], start=True, stop=True), [QK], [PS])
                        P.add("scalar", lambda e, PS=PS, PB=PB, c0=c0: e.activation(out=PB[:, c0:512], in_=PS[:, c0:512], func=AF.Exp),
                              [PS], [PB])
                        if J >= 4 * qt:
                            P.add("gpsimd", lambda e, PB=PB, c0=c0: e.tensor_tensor(
                                out=PB[:, c0:c0 + 128], in0=PB[:, c0:c0 + 128], in1=cmaskb[:], op=ALU.mult), [PB, cmaskb], [PB])
                        P.add("tensor", lambda e, PB=PB, PO=PO, h=h, J=J, c0=c0, nj=nj: e.matmul(
                            out=PO[:, c0:512], lhsT=VA[:, J, h, :], rhs=PB[:, c0:512], start=(J == 0), stop=(J == nj - 1)),
                            [VA, PB], [PO])
                    P.add("vector", lambda e, OS=OS, PO=PO: e.tensor_copy(out=OS[:], in_=PO[:]), [PO], [OS])
                    for k in range(4):
                        P.add("tensor", lambda e, k=k, OS=OS: e.transpose(out=pOT[:, k, :], in_=OS[:, k * 128:(k + 1) * 128],
                                                                         identity=identf[0:65, 0:65]), [OS, identf], [pOT])
                    P.add("vector", lambda e: e.reciprocal(out=rd[:], in_=pOT[:, :, 64]), [pOT], [rd])
                    for k in range(4):
                        P.add("vector", lambda e, k=k: e.tensor_scalar(out=Ot[:, k, :], in0=pOT[:, k, 0:64], scalar1=rd[:, k:k + 1],
                                                                       scalar2=None, op0=ALU.mult), [pOT, rd], [Ot])
                    for k in range(4):
                        P.add("scalar", lambda e, k=k: e.activation(out=junk[:], in_=Ot[:, k, :], func=AF.Square,
                                                                    accum_out=s4[:, k:k + 1]), [Ot], [junk, s4])
                    P.add("vector", lambda e: e.tensor_scalar(out=s4[:], in0=s4[:], scalar1=1.0 / 64, scalar2=EPS, op0=ALU.mult,
                                                              op1=ALU.add), [s4], [s4])
                    P.add("scalar", lambda e: e.activation(out=s4[:], in_=s4[:], func=AF.Sqrt), [s4], [s4])
                    P.add("vector", lambda e: e.reciprocal(out=s4[:], in_=s4[:]), [s4], [s4])
                    for k in range(4):
                        P.add("vector", lambda e, k=k, h=h, Y=Y: e.scalar_tensor_tensor(
                            out=Y[:, k, h * 64:(h + 1) * 64], in0=Ot[:, k, :], scalar=s4[:, k:k + 1], in1=gout[:, h * 64:(h + 1) * 64],
                            op0=ALU.mult, op1=ALU.mult), [Ot, s4, gout], [Y])
                YS = Yst[qt % 2]
                for k in range(4):
                    P.add("tensor", lambda e, k=k, Y=Y: e.transpose(out=pYT[:, k * 128:(k + 1) * 128], in_=Y[:, k, :], identity=identb[:]),
                          [Y, identb], [pYT])
                P.add("vector", lambda e, YS=YS: e.tensor_copy(out=YS[:], in_=pYT[:]), [pYT], [YS])
                P.dma(yT[0:128, qt * 512:(qt + 1) * 512], YS[:], reads=[YS], writes=[r_yT])
        P.barrier()
        if stop == 3:
            P.emit()
            return nc

        st_fox.close()
        with contextlib.ExitStack() as st4:
            C4 = Ctx(nc, st4)
            SEG = 1024 if T >= 1024 else T
            nseg = T // SEG
            tps = SEG // 128
            convw = C4.sb("convw", [64, 8])
            lbl = C4.sb("lbl", [64, 2])
            lsel = C4.sb("lsel", [64, 1])
            lb = C4.sb("lb", [64, 1])
            oml = C4.sb("oml", [64, 1])
            rmask = C4.sb("rmask", [64, SEG])
            P.dma(convw[:], convw_d[:, :], writes=[convw])
            P.dma(lbl[:], lbl_d[:, :], writes=[lbl])
            P.dma(lsel[:], lsel_d[:, :], writes=[lsel])
            P.dma(rmask[:], rmask_d[:, 0:SEG], writes=[rmask])
            P.add("vector", lambda e: e.tensor_tensor(out=lb[:], in0=lbl[:, 1:2], in1=lbl[:, 0:1], op=ALU.subtract), [lbl], [lb])
            P.add("scalar", lambda e: e.activation(out=lb[:], in_=lb[:], func=AF.Sigmoid), [lb], [lb])
            P.add("vector", lambda e: e.tensor_tensor(out=lb[:], in0=lb[:], in1=lsel[:], op=ALU.mult), [lb, lsel], [lb])
            P.add("vector", lambda e: e.tensor_scalar(out=oml[:], in0=lb[:], scalar1=-1.0, scalar2=1.0, op0=ALU.mult, op1=ALU.add),
                  [lb], [oml])
            zq = C4.sb("zq", [64, SEG + 3])
            zk = C4.sb("zk", [64, SEG + 3])
            cq = C4.sb("cq", [64, SEG])
            ck = C4.sb("ck", [64, SEG])
            qTb = C4.sb("qTb", [64, SEG], BF16)
            kTb = C4.sb("kTb", [64, SEG], BF16)
            zcq = C4.sb("zcq", [64, SEG])
            zcf = C4.sb("zcf", [64, SEG])
            kk = C4.sb("kk", [64, SEG])
            bcs = C4.sb("bcs", [64, SEG])
            e1 = C4.sb("e1", [64, SEG])
            e2 = C4.sb("e2", [64, SEG])
            e3 = C4.sb("e3", [64, SEG])
            qtl = C4.sb("qtl", [64, SEG], BF16)
            ktl = C4.sb("ktl", [64, SEG], BF16)
            kht = C4.sb("kht", [64, SEG], BF16)
            Cst = C4.sb("Cst", [64, 65])
            Cbf = C4.sb("Cbf", [64, 65], BF16)
            Sst = C4.sb("Sst", [64, 64])
            Sbf = [C4.sb("Sbf%d" % i, [64, 64], BF16) for i in range(2)]
            P.add("vector", lambda e: e.memset(Cst[:], 0.0), [], [Cst])
            P.add("vector", lambda e: e.memset(Cbf[:], 0.0), [], [Cbf])
            P.add("vector", lambda e: e.memset(Sst[:], 0.0), [], [Sst])
            P.add("vector", lambda e: e.memset(Sbf[0][:], 0.0), [], [Sbf[0]])
            LW = [C4.sb("LW%d" % i, [2, 128]) for i in range(2)]
            RW = [C4.sb("RW%d" % i, [2, 128]) for i in range(2)]
            for i in range(2):
                P.add("vector", lambda e, i=i: e.memset(LW[i][:], 1.0), [], [LW[i]])
                P.add("vector", lambda e, i=i: e.memset(RW[i][:], 1.0), [], [RW[i]])
            pST = C4.ps("pST", [128, 128])
            pLW = C4.ps("pLW", [128, 128])
            pI = C4.ps("pI", [128, 2, 128])
            pKT = C4.ps("pKT", [128, 2, 64], BF16)
            pDC = C4.ps("pDC", [64, 3, 65])
            pAT = C4.ps("pAT", [128, 128])
            pOc = C4.ps("pOc", [128, 64])
            pYT2 = C4.ps("pYT2", [128, 128], BF16)
            Dm = C4.sb("Dm", [128, 128])
            Wm = C4.sb("Wm", [128, 128], BF16)
            insb = C4.sb("insb", [128, 65])
            tot = C4.sb("tot", [128, 65])
            den = C4.sb("den", [128, 1])
            hb = C4.sb("hb", [128, 64])
            oc = C4.sb("oc", [128, 64])
            s1 = C4.sb("s1", [128, 2])
            junk2 = C4.sb("junk2", [128, 64])
            gg = C4.sb("gg", [128, 128])
            kwt = C4.sb("kwt", [128, 64], BF16)
            khA = C4.sb("khA", [128, 64], BF16)
            khB = C4.sb("khB", [128, 64], BF16)
            pm = C4.sb("pm", [128, 2])
            P.add("vector", lambda e: e.tensor_copy(out=pm[:, 0:1], in_=cmask[:, 63:64]), [cmask], [pm])
            P.add("vector", lambda e: e.tensor_scalar(out=pm[:, 1:2], in0=cmask[:, 63:64], scalar1=-1.0, scalar2=1.0, op0=ALU.mult, op1=ALU.add),
                  [cmask, pm], [pm])
            hmask = C4.sb("hmask", [64, SEG])
            P.dma(hmask[:], hmask_d[:, 0:SEG], writes=[hmask])
            qtlA = C4.sb("qtlA", [64, SEG], BF16)
            qtlB = C4.sb("qtlB", [64, SEG], BF16)
            ATs = C4.sb("ATs", [128, 128], BF16)
            Ybc = [C4.sb("Ybc%d" % i, [128, 128], BF16) for i in range(2)]
            Ys2 = [C4.sb("Ys2%d" % i, [128, 128]) for i in range(2)]

            for sg in range(nseg):
                t0 = sg * SEG
                P.dma(zq[:], zfm_d[0, :, 4 + t0 - 3:4 + t0 + SEG], reads=[r_zfm], writes=[zq])
                P.dma(zk[:], zfm_d[1, :, 4 + t0 - 3:4 + t0 + SEG], reads=[r_zfm], writes=[zk])
                P.dma(zcq[:], zfm_d[2, :, 4 + t0:4 + t0 + SEG], reads=[r_zfm], writes=[zcq])
                P.dma(zcf[:], zfm_d[3, :, 4 + t0:4 + t0 + SEG], reads=[r_zfm], writes=[zcf])
                for (zz, cc, wo, eng) in ((zq, cq, 0, "vector"), (zk, ck, 4, "vector")):
                    P.add(eng, lambda e, zz=zz, cc=cc, wo=wo: e.tensor_scalar(
                        out=cc[:], in0=zz[:, 3:SEG + 3], scalar1=convw[:, wo + 3:wo + 4], scalar2=None, op0=ALU.mult),
                        [zz, convw], [cc])
                    for kq in range(3):
                        P.add(eng, lambda e, zz=zz, cc=cc, wo=wo, kq=kq: e.scalar_tensor_tensor(
                            out=cc[:], in0=zz[:, kq:SEG + kq], scalar=convw[:, wo + kq:wo + kq + 1], in1=cc[:],
                            op0=ALU.mult, op1=ALU.add), [zz, convw, cc], [cc])
                P.add("scalar", lambda e: e.activation(out=qTb[:], in_=cq[:], func=AF.Silu), [cq], [qTb])
                P.add("scalar", lambda e: e.activation(out=ck[:], in_=ck[:], func=AF.Silu), [ck], [ck])
                P.add("gpsimd", lambda e: e.tensor_scalar(out=kTb[:], in0=ck[:], scalar1=0.125, scalar2=None, op0=ALU.mult),
                      [ck], [kTb])
                P.add("scalar", lambda e: e.activation(out=zcq[:], in_=zcq[:], func=AF.Silu), [zcq], [zcq])
                P.add("scalar", lambda e: e.activation(out=zcf[:], in_=zcf[:], func=AF.Sigmoid), [zcf], [zcf])
                P.add("vector", lambda e: e.tensor_scalar(out=zcf[:], in0=zcf[:], scalar1=oml[:, 0:1], scalar2=lb[:, 0:1],
                                                          op0=ALU.mult, op1=ALU.add), [zcf, oml, lb], [zcf])
                P.add("vector", lambda e: e.tensor_scalar(out=kk[:], in0=zcf[:], scalar1=-1.0, scalar2=1.0, op0=ALU.mult, op1=ALU.add),
                      [zcf], [kk])
                P.add("scalar", lambda e: e.activation(out=zcf[:], in_=zcf[:], func=AF.Ln), [zcf], [zcf])
                P.add("vector", lambda e: e.tensor_tensor_scan(out=bcs[:], data0=rmask[:], data1=zcf[:], initial=0.0,
                                                               op0=ALU.mult, op1=ALU.add), [rmask, zcf], [bcs])
                P.add("scalar", lambda e: e.activation(out=e1[:], in_=bcs[:], func=AF.Exp), [bcs], [e1])
                P.add("scalar", lambda e: e.activation(out=e2[:], in_=bcs[:], func=AF.Exp, scale=-1.0), [bcs], [e2])
                nch = SEG // 64
                P.add("vector", lambda e: e.tensor_tensor(
                    out=e3[:].rearrange("d (c t) -> d c t", t=64), in0=bcs[:].rearrange("d (c t) -> d c t", t=64)[:, :, 63:64].to_broadcast([64, nch, 64]),
                    in1=bcs[:].rearrange("d (c t) -> d c t", t=64), op=ALU.subtract), [bcs], [e3])
                P.add("scalar", lambda e: e.activation(out=e3[:], in_=e3[:], func=AF.Exp), [e3], [e3])
                P.add("vector", lambda e: e.tensor_tensor(out=qtl[:], in0=zcq[:], in1=e1[:], op=ALU.mult), [zcq, e1], [qtl])
                P.add("gpsimd", lambda e: e.tensor_tensor(out=ktl[:], in0=kk[:], in1=e2[:], op=ALU.mult), [kk, e2], [ktl])
                P.add("gpsimd", lambda e: e.tensor_tensor(out=qtlA[:], in0=qtl[:], in1=hmask[:], op=ALU.mult), [qtl, hmask], [qtlA])
                P.add("gpsimd", lambda e: e.tensor_tensor(out=qtlB[:], in0=qtl[:], in1=qtlA[:], op=ALU.subtract), [qtl, qtlA], [qtlB])
                P.add("gpsimd", lambda e: e.tensor_tensor(out=kht[:], in0=kk[:], in1=e3[:], op=ALU.mult), [kk, e3], [kht])

                if stop == 4:
                    continue
                for ti in range(tps):
                    t = sg * tps + ti
                    sl = slice(ti * 128, (ti + 1) * 128)
                    lw, rw = LW[t % 2], RW[t % 2]
                    Y = Ybc[t % 2]
                    P.dma(lw[0:1, :], rows_d[0:1, t * 128:(t + 1) * 128], reads=[r_rows], writes=[lw])
                    P.dma(rw[1:2, :], rows_d[1:2, t * 128:(t + 1) * 128], reads=[r_rows], writes=[rw])
                    P.add("tensor", lambda e, sl=sl: e.matmul(out=pST[:], lhsT=kTb[:, sl], rhs=qTb[:, sl], start=True, stop=True),
                          [kTb, qTb], [pST])
                    P.add("tensor", lambda e, lw=lw, rw=rw: e.matmul(out=pLW[:], lhsT=lw[:], rhs=rw[:], start=True, stop=True),
                          [lw, rw], [pLW])
                    P.add("scalar", lambda e: e.activation(out=Dm[:], in_=pLW[:], func=AF.Exp), [pLW], [Dm])
                    P.add("gpsimd", lambda e: e.tensor_tensor(out=Dm[:], in0=Dm[:], in1=cmask[:], op=ALU.mult), [Dm, cmask], [Dm])
                    P.add("vector", lambda e: e.tensor_tensor(out=Wm[:], in0=pST[:], in1=Dm[:], op=ALU.mult), [pST, Dm], [Wm])
                    P.add("tensor", lambda e, t=t: e.matmul(out=pI[:, 0, 0:65], lhsT=Wm[:], rhs=VB[:, t, :], start=True, stop=True),
                          [Wm, VB], [pI])
                    P.add("tensor", lambda e, sl=sl: e.matmul(out=pI[:, 1, 0:65], lhsT=qTb[:, sl], rhs=Cbf[:], start=True, stop=True),
                          [qTb, Cbf], [pI])
                    P.add("scalar", lambda e: e.copy(out=insb[:], in_=pI[:, 0, 0:65]), [pI], [insb])
                    P.add("vector", lambda e, t=t: e.scalar_tensor_tensor(out=tot[:], in0=pI[:, 1, 0:65], scalar=IW[:, t:t + 1], in1=insb[:],
                                                                          op0=ALU.mult, op1=ALU.add), [pI, IW, insb], [tot])
                    P.add("vector", lambda e: e.tensor_scalar(out=den[:], in0=tot[:, 64:65], scalar1=-1.0, scalar2=None, op0=ALU.mult),
                          [tot], [den])
                    P.add("vector", lambda e: e.tensor_tensor(out=den[:], in0=den[:], in1=tot[:, 64:65], op=ALU.max), [tot, den], [den])
                    P.add("vector", lambda e, t=t: e.tensor_scalar(out=den[:], in0=den[:], scalar1=EM[:, t:t + 1], scalar2=None,
                                                                   op0=ALU.max), [den, EM], [den])
                    P.add("vector", lambda e: e.reciprocal(out=den[:], in_=den[:]), [den], [den])
                    P.add("vector", lambda e: e.tensor_scalar(out=hb[:], in0=tot[:, 0:64], scalar1=den[:, 0:1], scalar2=None,
                                                              op0=ALU.mult), [tot, den], [hb])
                    if stop == 5:
                        continue
                    P.add("tensor", lambda e, sl=sl: e.transpose(out=pKT[:, 0, :], in_=kTb[:, sl], identity=identb[0:64, 0:64]),
                          [kTb, identb], [pKT])
                    P.add("tensor", lambda e, sl=sl: e.transpose(out=pKT[:, 1, :], in_=kht[:, sl], identity=identb[0:64, 0:64]),
                          [kht, identb], [pKT])
                    P.add("vector", lambda e, t=t: e.tensor_scalar(out=kwt[:], in0=pKT[:, 0, :], scalar1=SW[:, t:t + 1], scalar2=None,
                                                                   op0=ALU.mult), [pKT, SW], [kwt])
                    P.add("vector", lambda e: e.tensor_scalar(out=khA[:], in0=pKT[:, 1, :], scalar1=pm[:, 0:1], scalar2=None, op0=ALU.mult),
                          [pKT, pm], [khA])
                    P.add("vector", lambda e: e.tensor_scalar(out=khB[:], in0=pKT[:, 1, :], scalar1=pm[:, 1:2], scalar2=None, op0=ALU.mult),
                          [pKT, pm], [khB])
                    P.add("tensor", lambda e, t=t: e.matmul(out=pDC[:, 0, :], lhsT=kwt[:], rhs=VB[:, t, :], start=True, stop=True),
                          [kwt, VB], [pDC])
                    P.add("tensor", lambda e, t=t: e.matmul(out=pDC[:, 1, 0:64], lhsT=khA[:], rhs=VC[:, t, :],
                                                           start=True, stop=True), [khA, VC], [pDC])
                    P.add("tensor", lambda e, t=t: e.matmul(out=pDC[:, 2, 0:64], lhsT=khB[:], rhs=VC[:, t, :],
                                                           start=True, stop=True), [khB, VC], [pDC])
                    if stop == 6:
                        continue
                    P.add("tensor", lambda e, sl=sl: e.matmul(out=pAT[:], lhsT=ktl[:, sl], rhs=qtl[:, sl], start=True, stop=True),
                          [ktl, qtl], [pAT])
                    P.add("vector", lambda e: e.tensor_tensor(out=ATs[:], in0=pAT[:], in1=bdmask[:], op=ALU.mult), [pAT, bdmask], [ATs])
                    P.add("tensor", lambda e, t=t: e.matmul(out=pOc[:], lhsT=ATs[:], rhs=VC[:, t, :], start=True, stop=False),
                          [ATs, VC], [pOc])
                    P.add("tensor", lambda e, sl=sl: e.matmul(out=pOc[:], lhsT=qtlA[:, sl], rhs=Sbf[0][:],
                                                             start=False, stop=False), [qtlA, Sbf[0]], [pOc])
                    c0l = ti * 128 + 63
                    c1l = ti * 128 + 127
                    P.add("vector", lambda e, c0l=c0l: e.scalar_tensor_tensor(out=Sst[:], in0=Sst[:], scalar=e1[:, c0l:c0l + 1],
                                                                              in1=pDC[:, 1, 0:64], op0=ALU.mult, op1=ALU.add),
                          [Sst, e1, pDC], [Sst])
                    P.add("gpsimd", lambda e: e.tensor_copy(out=Sbf[1][:], in_=Sst[:]), [Sst], [Sbf[1]])
                    P.add("tensor", lambda e, sl=sl: e.matmul(out=pOc[:], lhsT=qtlB[:, sl], rhs=Sbf[1][:],
                                                             start=False, stop=True), [qtlB, Sbf[1]], [pOc])
                    P.add("vector", lambda e, c1l=c1l: e.scalar_tensor_tensor(out=Sst[:], in0=Sst[:], scalar=e1[:, c1l:c1l + 1],
                                                                              in1=pDC[:, 2, 0:64], op0=ALU.mult, op1=ALU.add),
                          [Sst, e1, pDC], [Sst])
                    P.add("gpsimd", lambda e: e.tensor_copy(out=Sbf[0][:], in_=Sst[:]), [Sst], [Sbf[0]])
                    P.add("vector", lambda e, t=t: e.scalar_tensor_tensor(out=Cst[:], in0=Cst[:], scalar=DEC[0:64, t:t + 1],
                                                                          in1=pDC[:, 0, :], op0=ALU.mult, op1=ALU.add),
                          [Cst, DEC, pDC], [Cst])
                    P.add("gpsimd", lambda e: e.tensor_copy(out=Cbf[:], in_=Cst[:]), [Cst], [Cbf])
                    if stop == 7:
                        continue
                    P.add("scalar", lambda e: e.copy(out=oc[:], in_=pOc[:]), [pOc], [oc])
                    P.add("scalar", lambda e: e.activation(out=junk2[:], in_=hb[:], func=AF.Square, accum_out=s1[:, 0:1]),
                          [hb], [junk2, s1])
                    P.add("scalar", lambda e: e.activation(out=junk2[:], in_=oc[:], func=AF.Square, accum_out=s1[:, 1:2]),
                          [oc, junk2], [junk2, s1])
                    P.add("vector", lambda e: e.tensor_scalar(out=s1[:], in0=s1[:], scalar1=1.0 / 64, scalar2=EPS, op0=ALU.mult,
                                                              op1=ALU.add), [s1], [s1])
                    P.add("scalar", lambda e: e.activation(out=s1[:], in_=s1[:], func=AF.Sqrt), [s1], [s1])
                    P.add("vector", lambda e: e.reciprocal(out=s1[:], in_=s1[:]), [s1], [s1])
                    P.add("gpsimd", lambda e, t=t: e.tensor_tensor(out=gg[:, 0:64], in0=gout[:, 128:192], in1=GB[:, t, :], op=ALU.mult),
                          [gout, GB, gg], [gg])
                    P.add("gpsimd", lambda e, t=t: e.tensor_tensor(out=gg[:, 64:128], in0=gout[:, 192:256], in1=GC[:, t, :], op=ALU.mult),
                          [gout, GC, gg], [gg])
                    P.add("vector", lambda e, Y=Y: e.scalar_tensor_tensor(out=Y[:, 0:64], in0=hb[:], scalar=s1[:, 0:1], in1=gg[:, 0:64],
                                                                          op0=ALU.mult, op1=ALU.mult), [hb, s1, gg], [Y])
                    P.add("vector", lambda e, Y=Y: e.scalar_tensor_tensor(out=Y[:, 64:128], in0=oc[:], scalar=s1[:, 1:2], in1=gg[:, 64:128],
                                                                          op0=ALU.mult, op1=ALU.mult), [oc, s1, gg, Y], [Y])
                    P.add("tensor", lambda e, Y=Y: e.transpose(out=pYT2[:], in_=Y[:], identity=identb[:]), [Y, identb], [pYT2])
                    YS = Ys2[t % 2]
                    P.add("vector", lambda e, YS=YS: e.tensor_copy(out=YS[:], in_=pYT2[:]), [pYT2], [YS])
                    P.dma(yT[128:256, t * 128:(t + 1) * 128], YS[:], reads=[YS], writes=[r_yT])
        P.emit()
    return nc


def _mixer_inputs(x, l, lb_logits, norm_mix_g, w_in, b_in, a_q_g, a_k_g, b_conv_w, out_g, T=SEQ):
    f = np.float32
    ident = np.eye(128, dtype=f)
    cmask = np.triu(np.ones((128, 128), f))
    bd = np.zeros((128, 128), f)
    bd[0:64, 0:64] = cmask[0:64, 0:64]
    bd[64:128, 64:128] = cmask[0:64, 0:64]
    ltri = np.triu(np.ones((64, 64), f), k=1)
    rmask = np.ones((64, 2048), f)
    rmask[:, 0::64] = 0.0
    hmask = np.tile(np.concatenate([np.ones(64, f), np.zeros(64, f)]), 16)[None, :].repeat(64, axis=0)
    maps = []
    W = w_in[l]
    B = b_in[l]
    for b in range(2):
        for g in range(4):
            aq = np.arange(128 * g, 128 * g + 128)
            tm_cols = np.concatenate([
                aq, 512 + aq, 1024 + aq,
                2056 + 64 * g + np.arange(64),
                2320 + 64 * g + np.arange(64),
                3088 + 64 * g + np.arange(64),
                3344 + 64 * g + np.arange(64),
                np.array([1536 + 2 * g, 1536 + 2 * g + 1, 2312 + g, 2316 + g])])
            fm_cols = np.concatenate([
                1544 + 64 * g + np.arange(64),
                1544 + 256 + 64 * g + np.arange(64),
                2576 + 64 * g + np.arange(64),
                2832 + 64 * g + np.arange(64)])
            bfm = np.repeat(B[fm_cols].reshape(4, 64).T[:, :, None], 128, axis=2).reshape(64, 512)
            gqk = np.concatenate([a_q_g[l], a_q_g[l], a_k_g[l], a_k_g[l]])
            cw = b_conv_w[l]
            convw = np.concatenate([cw[:, 64 * g:64 * g + 64].T, cw[:, 256 + 64 * g:256 + 64 * g + 64].T], axis=1)
            og = out_g[l]
            gout = np.concatenate([og[128 * g:128 * g + 128], og[512 + 64 * g:512 + 64 * g + 64],
                                   og[768 + 64 * g:768 + 64 * g + 64]])
            maps.append({
                "xb": np.ascontiguousarray(x[b, :T]),
                "wtm": np.ascontiguousarray(W[:, tm_cols]),
                "wfm": np.ascontiguousarray(W[:, fm_cols]),
                "gmix": np.ascontiguousarray(norm_mix_g[l].reshape(8, 128).T),
                "btm": np.ascontiguousarray(np.broadcast_to(B[tm_cols][None, :], (128, NTM))),
                "bfm": np.ascontiguousarray(bfm),
                "gqk": np.ascontiguousarray(np.broadcast_to(gqk[None, :], (128, 256))),
                "convw": np.ascontiguousarray(convw),
                "gout": np.ascontiguousarray(np.broadcast_to(gout[None, :], (128, 256))),
                "lbl": np.ascontiguousarray(lb_logits[:, 64 * g:64 * g + 64].T),
                "lsel": np.full((64, 1), float(l), f),
                "ident": ident, "cmask": cmask, "bdmask": bd, "ltri": ltri, "rmask": rmask, "hmask": np.ascontiguousarray(hmask),
            })
    return maps


NJ = DFF // 128


def build_ffn(NTOK=2048):
    NTT = NTOK + 128
    NTL = NTT // 128
    nc = bass.Bass("TRN2", target_bir_lowering=False)

    def din(name, shape):
        return nc.dram_tensor(name, list(shape), F32, kind="ExternalInput").ap()

    yTs = din("yTs", [DM, NTT])
    xs = din("xs", [NTT, DM])
    wout_d = din("wout", [DM, DM])
    gffn_d = din("gffn", [128, 8 * 128])
    wup_d = din("wup", [DM, 2 * DFF])
    cw_d = din("cw", [128, NJ * 2 * 3])
    cb_d = din("cb", [128, NJ * 2])
    wdn_d = din("wdn", [DFF, DM])
    ident_d = din("ident", [128, 128])
    xo = nc.dram_tensor("xo", [NTOK, DM], F32, kind="ExternalOutput").ap()
    x1_d = nc.dram_tensor("x1_s", [NTOK, DM], F32).ap()
    r_x1 = Res("x1_d")
    r_xo = Res("xo")

    P = Prog(nc)
    with contextlib.ExitStack() as st_all:
        G = Ctx(nc, st_all)
        identf = G.sb("identf", [128, 128])
        identb = G.sb("identb", [128, 128], BF16)
        P.dma(identf[:], ident_d[:, :], writes=[identf])
        P.add("vector", lambda e: e.tensor_copy(out=identb[:], in_=identf[:]), [identf], [identb])
        aT = G.sb("aT", [128, NJ, NTOK], BF16)
        with contextlib.ExitStack() as st12:
            C12 = Ctx(nc, st12)
            h2T = C12.sb("h2T", [128, 8, NTT], BF16)
            with contextlib.ExitStack() as st1:
                C1 = Ctx(nc, st1)
                woutb = C1.sb("woutb", [128, 8, DM], BF16)
                wst = [C1.sb("wst%d" % i, [128, DM]) for i in range(2)]
                for kc in range(8):
                    s = wst[kc % 2]
                    P.dma(s[:], wout_d[kc * 128:(kc + 1) * 128, :], writes=[s])
                    P.add("gpsimd", lambda e, s=s, kc=kc: e.tensor_copy(out=woutb[:, kc, :], in_=s[:]), [s], [woutb])
                ytf = [C1.sb("ytf%d" % i, [128, 8, 128]) for i in range(2)]
                ytb = [C1.sb("ytb%d" % i, [128, 8, 128], BF16) for i in range(2)]
                xt = [C1.sb("xt%d" % i, [128, DM]) for i in range(2)]
                x1 = [C1.sb("x1%d" % i, [128, DM]) for i in range(2)]
                sqj = C1.sb("sqj", [128, DM])
                ss = [C1.sb("ss%d" % i, [128, 1]) for i in range(2)]
                h2 = [C1.sb("h2%d" % i, [128, DM], BF16) for i in range(2)]
                pA = [C1.ps("pA%d" % i, [128, 512]) for i in range(2)]
                pT = [C1.ps("pT%d" % i, [128, DM], BF16) for i in range(2)]

                def load(i):
                    P.dma(ytf[i % 2][:], yTs[:, i * 128:(i + 1) * 128].rearrange("(kc p) t -> p kc t", p=128), writes=[ytf[i % 2]])
                    P.dma(xt[i % 2][:], xs[i * 128:(i + 1) * 128, :], writes=[xt[i % 2]])

                load(0)
                for i in range(NTL):
                    b = i % 2
                    if i + 1 < NTL:
                        load(i + 1)
                    YF, YB, X, X1, SS, H2, PT = ytf[b], ytb[b], xt[b], x1[b], ss[b], h2[b], pT[b]
                    P.add("gpsimd", lambda e, YF=YF, YB=YB: e.tensor_copy(out=YB[:], in_=YF[:]), [YF], [YB])
                    for hf in range(2):
                        PA = pA[hf]
                        for kc in range(8):
                            P.add("tensor", lambda e, kc=kc, hf=hf, YB=YB, PA=PA: e.matmul(
                                out=PA[:], lhsT=YB[:, kc, :], rhs=woutb[:, kc, hf * 512:(hf + 1) * 512],
                                start=(kc == 0), stop=(kc == 7)), [YB, woutb], [PA])
                        P.add("vector", lambda e, hf=hf, X=X, X1=X1, PA=PA: e.tensor_tensor(
                            out=X1[:, hf * 512:(hf + 1) * 512], in0=PA[:], in1=X[:, hf * 512:(hf + 1) * 512], op=ALU.add),
                            [PA, X, X1], [X1])
                    if i >= 1:
                        P.dma(x1_d[(i - 1) * 128:i * 128, :], X1[:], reads=[X1], writes=[r_x1])
                    P.add("scalar", lambda e, X1=X1, SS=SS: e.activation(out=sqj[:], in_=X1[:], func=AF.Square, accum_out=SS[:]),
                          [X1], [sqj, SS])
                    P.add("vector", lambda e, SS=SS: e.tensor_scalar(out=SS[:], in0=SS[:], scalar1=1.0 / DM, scalar2=EPS,
                                                                     op0=ALU.mult, op1=ALU.add), [SS], [SS])
                    P.add("scalar", lambda e, SS=SS: e.activation(out=SS[:], in_=SS[:], func=AF.Sqrt), [SS], [SS])
                    P.add("vector", lambda e, SS=SS: e.reciprocal(out=SS[:], in_=SS[:]), [SS], [SS])
                    P.add("vector", lambda e, X1=X1, SS=SS, H2=H2: e.tensor_scalar(
                        out=H2[:], in0=X1[:], scalar1=SS[:, 0:1], scalar2=None, op0=ALU.mult), [X1, SS], [H2])
                    for kc in range(8):
                        P.add("tensor", lambda e, kc=kc, H2=H2, PT=PT: e.transpose(
                            out=PT[:, kc * 128:(kc + 1) * 128], in_=H2[:, kc * 128:(kc + 1) * 128], identity=identb[:]),
                            [H2, identb], [PT])
                    P.add("scalar", lambda e, i=i, PT=PT: e.copy(out=h2T[:, :, i * 128:(i + 1) * 128],
                                                                 in_=PT[:].rearrange("p (kc t) -> p kc t", t=128)), [PT], [h2T])
            P.barrier()
            with contextlib.ExitStack() as st2:
                C2 = Ctx(nc, st2)
                gffn = C2.sb("gffn", [128, 8 * 128])
                cw = C2.sb("cw", [128, NJ * 2 * 3])
                cb = C2.sb("cb", [128, NJ * 2])
                P.dma(gffn[:], gffn_d[:, :], writes=[gffn])
                P.dma(cw[:], cw_d[:, :], writes=[cw])
                P.dma(cb[:], cb_d[:, :], writes=[cb])
                wf = [[C2.sb("wf%d%d" % (i, k), [128, 8, 128]) for k in range(2)] for i in range(2)]
                wb = [[C2.sb("wb%d%d" % (i, k), [128, 8, 128], BF16) for k in range(2)] for i in range(2)]
                usb = [C2.sb("usb%d" % k, [128, NTT]) for k in range(2)]
                cv = [C2.sb("cv%d" % k, [128, NTOK]) for k in range(2)]
                pU = [C2.ps("pU%d" % i, [128, 512]) for i in range(4)]
                groups = [(0, 128)] + [(128 + 512 * k, min(128 + 512 * (k + 1), NTT)) for k in range((NTOK + 511) // 512)]

                def loadw(j):
                    for k in range(2):
                        c0 = k * DFF + j * 128
                        P.dma(wf[j % 2][k][:], wup_d[:, c0:c0 + 128].rearrange("(kc p) c -> p kc c", p=128), writes=[wf[j % 2][k]])

                loadw(0)
                pi = 0
                for j in range(NJ):
                    if j + 1 < NJ:
                        loadw(j + 1)
                    for k in range(2):
                        WF, WB = wf[j % 2][k], wb[j % 2][k]
                        P.add("gpsimd", lambda e, WF=WF, WB=WB: e.tensor_tensor(
                            out=WB[:].rearrange("p kc c -> p (kc c)"), in0=WF[:].rearrange("p kc c -> p (kc c)"), in1=gffn[:], op=ALU.mult),
                            [WF, gffn], [WB])
                    for (g0, g1) in groups:
                        for k in range(2):
                            WB = wb[j % 2][k]
                            PU = pU[pi % 4]
                            pi += 1
                            n = g1 - g0
                            for kc in range(8):
                                P.add("tensor", lambda e, kc=kc, WB=WB, PU=PU, g0=g0, g1=g1, n=n: e.matmul(
                                    out=PU[:, 0:n], lhsT=WB[:, kc, :], rhs=h2T[:, kc, g0:g1], start=(kc == 0), stop=(kc == 7)),
                                    [WB, h2T], [PU])
                            if k == 0:
                                P.add("scalar", lambda e, PU=PU, g0=g0, g1=g1, n=n: e.copy(out=usb[0][:, g0:g1], in_=PU[:, 0:n]),
                                      [PU], [usb[0]])
                            else:
                                P.add("vector", lambda e, PU=PU, g0=g0, g1=g1, n=n: e.tensor_copy(out=usb[1][:, g0:g1], in_=PU[:, 0:n]),
                                      [PU], [usb[1]])
                    for k in range(2):
                        U, CV = usb[k], cv[k]
                        wi = (j * 2 + k) * 3
                        bi = j * 2 + k
                        P.add("vector", lambda e, U=U, CV=CV, wi=wi, bi=bi: e.tensor_scalar(
                            out=CV[:], in0=U[:, 128:NTT], scalar1=cw[:, wi + 2:wi + 3], scalar2=cb[:, bi:bi + 1], op0=ALU.mult, op1=ALU.add),
                            [U, cw, cb], [CV])
                        P.add("vector", lambda e, U=U, CV=CV, wi=wi: e.scalar_tensor_tensor(
                            out=CV[:], in0=U[:, 127:NTT - 1], scalar=cw[:, wi + 1:wi + 2], in1=CV[:], op0=ALU.mult, op1=ALU.add),
                            [U, cw, CV], [CV])
                        P.add("vector", lambda e, U=U, CV=CV, wi=wi: e.scalar_tensor_tensor(
                            out=CV[:], in0=U[:, 126:NTT - 2], scalar=cw[:, wi:wi + 1], in1=CV[:], op0=ALU.mult, op1=ALU.add),
                            [U, cw, CV], [CV])
                    P.add("scalar", lambda e: e.activation(out=cv[0][:], in_=cv[0][:], func=AF.Silu), [cv[0]], [cv[0]])
                    P.add("gpsimd", lambda e, j=j: e.tensor_tensor(out=aT[:, j, :], in0=cv[0][:], in1=cv[1][:], op=ALU.mult),
                          [cv[0], cv[1]], [aT])
            P.barrier()
        with contextlib.ExitStack() as st3:
            C3 = Ctx(nc, st3)
            wd = C3.sb("wd", [128, NJ, DM], BF16)
            wst = [C3.sb("wst%d" % i, [128, DM]) for i in range(2)]
            for j in range(NJ):
                s = wst[j % 2]
                P.dma(s[:], wdn_d[j * 128:(j + 1) * 128, :], writes=[s])
                P.add("gpsimd", lambda e, s=s, j=j: e.tensor_copy(out=wd[:, j, :], in_=s[:]), [s], [wd])
            x1t = [C3.sb("x1t%d" % i, [128, DM]) for i in range(2)]
            ot = [C3.sb("ot%d" % i, [128, DM]) for i in range(2)]
            pD = [C3.ps("pD%d" % i, [128, 512]) for i in range(4)]
            P.dma(x1t[0][:], x1_d[0:128, :], reads=[r_x1], writes=[x1t[0]])
            for i in range(NTOK // 128):
                if i + 1 < NTOK // 128:
                    P.dma(x1t[(i + 1) % 2][:], x1_d[(i + 1) * 128:(i + 2) * 128, :], reads=[r_x1], writes=[x1t[(i + 1) % 2]])
                X1, O = x1t[i % 2], ot[i % 2]
                for hf in range(2):
                    PD = pD[(2 * i + hf) % 4]
                    for j in range(NJ):
                        P.add("tensor", lambda e, j=j, i=i, hf=hf, PD=PD: e.matmul(
                            out=PD[:], lhsT=aT[:, j, i * 128:(i + 1) * 128], rhs=wd[:, j, hf * 512:(hf + 1) * 512],
                            start=(j == 0), stop=(j == NJ - 1)), [aT, wd], [PD])
                    P.add("vector", lambda e, hf=hf, PD=PD, X1=X1, O=O: e.tensor_tensor(
                        out=O[:, hf * 512:(hf + 1) * 512], in0=PD[:], in1=X1[:, hf * 512:(hf + 1) * 512], op=ALU.add),
                        [PD, X1, O], [O])
                P.dma(xo[i * 128:(i + 1) * 128, :], O[:], reads=[O], writes=[r_xo])
        P.emit()
    return nc


def _ffn_inputs(xfull, yT_cores, l, w_out, norm_ffn_g, w_up, ffn_conv_w, ffn_conv_b, w_down, T=SEQ, NTOK=2048):
    f = np.float32
    ident = np.eye(128, dtype=f)
    nq = T // NTOK
    cwl = ffn_conv_w[l]
    cbl = ffn_conv_b[l]
    cw = np.zeros((128, NJ, 2, 3), f)
    cb = np.zeros((128, NJ, 2), f)
    for k in range(2):
        cw[:, :, k, :] = cwl[:, k * DFF:(k + 1) * DFF].T.reshape(NJ, 128, 3).transpose(1, 0, 2)
        cb[:, :, k] = cbl[k * DFF:(k + 1) * DFF].reshape(NJ, 128).T
    gffn = np.repeat(norm_ffn_g[l].reshape(8, 128).T[:, :, None], 128, axis=2).reshape(128, 8 * 128)
    maps = []
    for b in range(2):
        yT = np.zeros((DM, 128 + T), f)
        for g in range(4):
            yc = yT_cores[b * 4 + g]
            yT[128 * g:128 * g + 128, 128:] = yc[0:128]
            yT[512 + 64 * g:512 + 64 * g + 64, 128:] = yc[128:192]
            yT[768 + 64 * g:768 + 64 * g + 64, 128:] = yc[192:256]
        xp = np.zeros((128 + T, DM), f)
        xp[128:] = xfull[b, :T]
        for q in range(nq):
            maps.append({
                "yTs": np.ascontiguousarray(yT[:, q * NTOK:q * NTOK + NTOK + 128]),
                "xs": np.ascontiguousarray(xp[q * NTOK:q * NTOK + NTOK + 128]),
                "wout": np.ascontiguousarray(w_out[l]),
                "gffn": np.ascontiguousarray(gffn),
                "wup": np.ascontiguousarray(w_up[l]),
                "cw": np.ascontiguousarray(cw.reshape(128, NJ * 2 * 3)),
                "cb": np.ascontiguousarray(cb.reshape(128, NJ * 2)),
                "wdn": np.ascontiguousarray(w_down[l]),
                "ident": ident,
            })
    return maps


_CACHE = {}


def _get(name, fn):
    if name not in _CACHE:
        _CACHE[name] = fn()
    return _CACHE[name]


def kernel(x, lb_logits, norm_mix_g, w_in, b_in, a_q_g, a_k_g, b_conv_w, out_g, w_out,
           norm_ffn_g, w_up, ffn_conv_w, ffn_conv_b, w_down):
    args = [np.asarray(a, dtype=np.float32) for a in (x, lb_logits, norm_mix_g, w_in, b_in, a_q_g, a_k_g, b_conv_w, out_g,
                                                       w_out, norm_ffn_g, w_up, ffn_conv_w, ffn_conv_b, w_down)]
    (x, lb_logits, norm_mix_g, w_in, b_in, a_q_g, a_k_g, b_conv_w, out_g, w_out, norm_ffn_g, w_up, ffn_conv_w, ffn_conv_b,
     w_down) = args
    cores = list(range(8))
    xc = x
    for l in range(2):
        nc_m = build_mixer(SEQ)
        maps = _mixer_inputs(xc, l, lb_logits, norm_mix_g, w_in, b_in, a_q_g, a_k_g, b_conv_w, out_g)
        res = run_bass_kernel_spmd(nc_m, maps, core_ids=cores)
        yT_cores = [r["yT"] for r in res.results]
        nc_f = build_ffn(2048)
        fmaps = _ffn_inputs(xc, yT_cores, l, w_out, norm_ffn_g, w_up, ffn_conv_w, ffn_conv_b, w_down)
        res2 = run_bass_kernel_spmd(nc_f, fmaps, core_ids=cores)
        xn = np.empty_like(xc)
        for b in range(2):
            for q in range(4):
                xn[b, q * 2048:(q + 1) * 2048] = res2.results[b * 4 + q]["xo"]
        xc = xn
    return xc
```
